# Optimizing a Trainium2 kernel written in Bass

```python
import math
import jax, jax.numpy as jnp
from jax import lax
import numpy as np

D_MODEL = 1024
BATCH = 8
SEQ = 2048
DEPTH = 4
DEC_BATCH = 2
DEC_SEQ = 16384
PAST_LEN = 128

N_META = 16
BLOCK = 128
PAD = BLOCK - N_META
RET_HEADS = 4
RET_HEAD_DIM = 128
RET_WIDTH = RET_HEADS * RET_HEAD_DIM
CONV_WIDTH = 512
CONV_TAPS = 3
ATT_Q_HEADS = 8
ATT_KV_HEADS = 2
ATT_HEAD_DIM = 64
ATT_GROUPS = ATT_Q_HEADS // ATT_KV_HEADS
ATT_WIDTH = ATT_Q_HEADS * ATT_HEAD_DIM
ATT_KV_WIDTH = ATT_KV_HEADS * ATT_HEAD_DIM
WINDOW = 128
N_BRANCH = 3
D_FF = 2816
ROPE_THETA = 10000.0
EPS = 1e-6
NEG_INF = -1e30
SPLITS = (RET_WIDTH, RET_WIDTH, RET_WIDTH, RET_WIDTH,
          CONV_WIDTH, CONV_WIDTH, CONV_WIDTH,
          ATT_WIDTH, ATT_KV_WIDTH, ATT_KV_WIDTH,
          N_BRANCH * D_MODEL)
IN_WIDTH = sum(SPLITS)

kernel_name = "hybrid_retention_conv_swa_encoder"


def rms_norm(x, g):
    xf = x.astype(jnp.float32)
    y = xf * lax.rsqrt(jnp.mean(xf * xf, axis=-1, keepdims=True) + EPS)
    return (y * g.astype(jnp.float32)).astype(x.dtype)


def swiglu(x, wg, wu, wd):
    return (jax.nn.silu(x @ wg) * (x @ wu)) @ wd


def rotary(x, pos):
    d = x.shape[-1]
    inv = ROPE_THETA ** (-jnp.arange(0, d, 2, dtype=jnp.float32) / d)
    ang = pos[:, None] * inv[None, :]
    cos = jnp.cos(ang)[None, :, None, :].astype(x.dtype)
    sin = jnp.sin(ang)[None, :, None, :].astype(x.dtype)
    x1, x2 = x[..., : d // 2], x[..., d // 2:]
    return jnp.concatenate([x1 * cos - x2 * sin, x2 * cos + x1 * sin], axis=-1)


def pad_front(t):
    return jnp.pad(t, [(0, 0), (PAD, 0)] + [(0, 0)] * (t.ndim - 2))


def retention_scan(q, k, v, log_gamma, strict):
    b, p, h, _ = q.shape
    dv = v.shape[-1]
    n = p // BLOCK
    idx = jnp.arange(BLOCK, dtype=jnp.float32)
    diff = idx[:, None] - idx[None, :]
    keep = (diff > 0) if strict else (diff >= 0)
    dmat = jnp.where(keep[None], jnp.exp(log_gamma[:, None, None] * jnp.maximum(diff, 0.0)[None]), 0.0)
    q_dec = jnp.exp(log_gamma[:, None] * (idx + 1.0)[None])
    k_dec = jnp.exp(log_gamma[:, None] * (BLOCK - 1.0 - idx)[None])
    c_dec = jnp.exp(log_gamma * BLOCK)

    def chunks(t):
        return t.reshape(b, n, BLOCK, h, t.shape[-1]).swapaxes(0, 1)

    def step(state, inp):
        qc, kc, vc = inp
        att = jnp.einsum('bihd,bjhd->bhij', qc, kc) * dmat[None]
        o = (jnp.einsum('bhij,bjhe->bihe', att, vc)
             + jnp.einsum('bihd,hi,bhde->bihe', qc, q_dec, state))
        state = (c_dec[None, :, None, None] * state
                 + jnp.einsum('bjhd,hj,bjhe->bhde', kc, k_dec, vc))
        return state, o

    s0 = jnp.zeros((b, h, q.shape[-1], dv), jnp.float32)
    _, o = lax.scan(step, s0, (chunks(q), chunks(k), chunks(v)))
    return o.swapaxes(0, 1).reshape(b, p, h, dv)


def retention(q, k, v, g, log_decay, gn_gain, pos):
    b, l, _ = q.shape
    dt = q.dtype
    heads = lambda t: t.reshape(b, l, RET_HEADS, RET_HEAD_DIM)
    qh = pad_front(rotary(heads(q), pos).astype(jnp.float32))
    kh = pad_front((rotary(heads(k), pos) * RET_HEAD_DIM ** -0.5).astype(jnp.float32))
    vh = pad_front(heads(v).astype(jnp.float32))
    log_gamma = jnp.log1p(-jnp.exp(log_decay.astype(jnp.float32)))
    fwd = retention_scan(qh, kh, vh, log_gamma[0], False)
    rev = lambda t: jnp.flip(t, axis=1)
    bwd = rev(retention_scan(rev(qh), rev(kh), rev(vh), log_gamma[1], True))
    o = (fwd + bwd)[:, PAD:]
    mu = jnp.mean(o, axis=-1, keepdims=True)
    var = jnp.mean(jnp.square(o - mu), axis=-1, keepdims=True)
    o = ((o - mu) * lax.rsqrt(var + EPS)).reshape(b, l, RET_WIDTH) * gn_gain.astype(jnp.float32)
    return (jax.nn.silu(g.astype(jnp.float32)) * o).astype(dt)


def short_conv(bg, cg, xc, w):
    u = cg * xc
    up = jnp.pad(u, ((0, 0), (1, 1), (0, 0)))
    y = up[:, :-2] * w[0] + up[:, 1:-1] * w[1] + up[:, 2:] * w[2]
    return bg * y


def window_attention(q, k, v, sink):
    b, l = q.shape[:2]
    q, k, v = pad_front(q), pad_front(k), pad_front(v)
    p = l + PAD
    nb = p // BLOCK
    qb = q.reshape(b, nb, BLOCK, ATT_KV_HEADS, ATT_GROUPS, ATT_HEAD_DIM)

    def neighbours(t):
        t = t.reshape(b, nb, BLOCK, ATT_KV_HEADS, ATT_HEAD_DIM)
        t = jnp.pad(t, ((0, 0), (1, 1), (0, 0), (0, 0), (0, 0)))
        return jnp.concatenate([t[:, :-2], t[:, 1:-1], t[:, 2:]], axis=2)

    kb, vb = neighbours(k), neighbours(v)
    s = jnp.einsum('bnqkgd,bnskd->bnkgqs', qb, kb).astype(jnp.float32) * (ATT_HEAD_DIM ** -0.5)
    start = jnp.arange(nb)[:, None] * BLOCK
    qpos = start + jnp.arange(BLOCK)[None, :]
    kpos = start - BLOCK + jnp.arange(3 * BLOCK)[None, :]
    kp = kpos[:, None, :]
    ok = (jnp.abs(qpos[:, :, None] - kp) <= WINDOW) & (kp >= PAD) & (kp < p)
    s = jnp.where(ok[None, :, None, None], s, NEG_INF)
    sink_col = jnp.broadcast_to(
        sink.astype(jnp.float32).reshape(1, 1, ATT_KV_HEADS, ATT_GROUPS, 1, 1), s.shape[:-1] + (1,))
    probs = jax.nn.softmax(jnp.concatenate([s, sink_col], axis=-1), axis=-1)[..., :-1]
    o = jnp.einsum('bnkgqs,bnskd->bnqkgd', probs.astype(v.dtype), vb)
    return o.reshape(b, p, ATT_WIDTH)[:, PAD:]


def mixer(h, w_in, ret_decay, ret_gn_gain, conv_w, attn_sink, w_ret_out, w_conv_out, w_attn_out, w_o, pos):
    b, l, _ = h.shape
    offsets = []
    acc = 0
    for s in SPLITS[:-1]:
        acc += s
        offsets.append(acc)
    rq, rk, rv, rg, cb, cc, cx, aq, ak, av, gates = jnp.split(h @ w_in, offsets, axis=-1)
    y_ret = retention(rq, rk, rv, rg, ret_decay, ret_gn_gain, pos) @ w_ret_out
    y_conv = short_conv(cb, cc, cx, conv_w) @ w_conv_out
    aq = rotary(aq.reshape(b, l, ATT_Q_HEADS, ATT_HEAD_DIM), pos)
    ak = rotary(ak.reshape(b, l, ATT_KV_HEADS, ATT_HEAD_DIM), pos)
    av = av.reshape(b, l, ATT_KV_HEADS, ATT_HEAD_DIM)
    y_att = window_attention(aq, ak, av, attn_sink) @ w_attn_out
    g = jax.nn.sigmoid(gates.reshape(b, l, N_BRANCH, D_MODEL))
    merged = g[:, :, 0] * y_ret + g[:, :, 1] * y_conv + g[:, :, 2] * y_att
    return merged @ w_o


def trunk(x, meta_tokens, norm_ffn1, w_ffn1_gate, w_ffn1_up, w_ffn1_down, norm_mix, w_in,
          ret_decay, ret_gn_gain, conv_w, attn_sink, w_ret_out, w_conv_out, w_attn_out, w_o,
          norm_ffn2, w_ffn2_gate, w_ffn2_up, w_ffn2_down, final_norm):
    b = x.shape[0]
    meta = jnp.broadcast_to(meta_tokens.astype(x.dtype)[None], (b, N_META, D_MODEL))
    h = jnp.concatenate([meta, x], axis=1)
    pos = jnp.arange(h.shape[1], dtype=jnp.float32)
    for l in range(DEPTH):
        h = h + 0.5 * swiglu(rms_norm(h, norm_ffn1[l]), w_ffn1_gate[l], w_ffn1_up[l], w_ffn1_down[l])
        h = h + mixer(rms_norm(h, norm_mix[l]), w_in[l], ret_decay[l], ret_gn_gain[l], conv_w[l],
                      attn_sink[l], w_ret_out[l], w_conv_out[l], w_attn_out[l], w_o[l], pos)
        h = h + 0.5 * swiglu(rms_norm(h, norm_ffn2[l]), w_ffn2_gate[l], w_ffn2_up[l], w_ffn2_down[l])
    return rms_norm(h, final_norm)[:, N_META:]


def setup_inputs(seed: int = 0) -> dict:
    key = jax.random.key(seed)
    ks = jax.random.split(key, 24)
    f32 = jnp.float32

    def dense(k, shape):
        return jax.random.normal(k, shape, f32) * (shape[-2] ** -0.5)

    def gain(k, shape):
        return 1.0 + 0.01 * jax.random.normal(k, shape, f32)

    base_decay = -(5.0 + jnp.arange(RET_HEADS, dtype=f32)) * math.log(2.0)
    return {
        "x_prompt": jax.random.normal(ks[0], (BATCH, SEQ, D_MODEL), f32),
        "x_sample": jax.random.normal(ks[1], (DEC_BATCH, DEC_SEQ, D_MODEL), f32),
        "meta_tokens": jax.random.normal(ks[2], (N_META, D_MODEL), f32),
        "norm_ffn1": gain(ks[3], (DEPTH, D_MODEL)),
        "w_ffn1_gate": dense(ks[4], (DEPTH, D_MODEL, D_FF)),
        "w_ffn1_up": dense(ks[5], (DEPTH, D_MODEL, D_FF)),
        "w_ffn1_down": dense(ks[6], (DEPTH, D_FF, D_MODEL)),
        "norm_mix": gain(ks[7], (DEPTH, D_MODEL)),
        "w_in": dense(ks[8], (DEPTH, D_MODEL, IN_WIDTH)),
        "ret_decay": base_decay[None, None, :] + 0.05 * jax.random.normal(ks[9], (DEPTH, 2, RET_HEADS), f32),
        "ret_gn_gain": gain(ks[10], (DEPTH, RET_WIDTH)),
        "conv_w": jax.random.normal(ks[11], (DEPTH, CONV_TAPS, CONV_WIDTH), f32) * (CONV_TAPS ** -0.5),
        "attn_sink": 0.5 * jax.random.normal(ks[12], (DEPTH, ATT_Q_HEADS), f32),
        "w_ret_out": dense(ks[13], (DEPTH, RET_WIDTH, D_MODEL)),
        "w_conv_out": dense(ks[14], (DEPTH, CONV_WIDTH, D_MODEL)),
        "w_attn_out": dense(ks[15], (DEPTH, ATT_WIDTH, D_MODEL)),
        "w_o": dense(ks[16], (DEPTH, D_MODEL, D_MODEL)),
        "norm_ffn2": gain(ks[17], (DEPTH, D_MODEL)),
        "w_ffn2_gate": dense(ks[18], (DEPTH, D_MODEL, D_FF)),
        "w_ffn2_up": dense(ks[19], (DEPTH, D_MODEL, D_FF)),
        "w_ffn2_down": dense(ks[20], (DEPTH, D_FF, D_MODEL)),
        "final_norm": gain(ks[21], (D_MODEL,)),
    }


def reference(x_prompt, x_sample, meta_tokens, norm_ffn1, w_ffn1_gate, w_ffn1_up, w_ffn1_down, norm_mix,
              w_in, ret_decay, ret_gn_gain, conv_w, attn_sink, w_ret_out, w_conv_out, w_attn_out, w_o,
              norm_ffn2, w_ffn2_gate, w_ffn2_up, w_ffn2_down, final_norm):
    y_prompt = trunk(x_prompt, meta_tokens, norm_ffn1, w_ffn1_gate, w_ffn1_up, w_ffn1_down, norm_mix, w_in,
                     ret_decay, ret_gn_gain, conv_w, attn_sink, w_ret_out, w_conv_out, w_attn_out, w_o,
                     norm_ffn2, w_ffn2_gate, w_ffn2_up, w_ffn2_down, final_norm)
    y_sample = trunk(x_sample, meta_tokens, norm_ffn1, w_ffn1_gate, w_ffn1_up, w_ffn1_down, norm_mix, w_in,
                     ret_decay, ret_gn_gain, conv_w, attn_sink, w_ret_out, w_conv_out, w_attn_out, w_o,
                     norm_ffn2, w_ffn2_gate, w_ffn2_up, w_ffn2_down, final_norm)
    return (y_prompt, y_sample)
```

```python
import numpy as np
import ml_dtypes
import concourse.bass as bass
import concourse.mybir as mybir
from concourse.bass_utils import run_bass_kernel_spmd

F32 = mybir.dt.float32
BF16 = mybir.dt.bfloat16
AF = mybir.ActivationFunctionType
ALU = mybir.AluOpType

NCORES = 8
D = 1024
KC = 8
DEPTH = 4
DFF = 2816
NJ = DFF // 128
NMETA = 16
EPS = 1e-6
INW = 7424
NRANK = 8
SB_LIMIT = 208 * 1024
SB_BASE = 16384

CB_IDENT, CB_ONES, CB_OMEAN, CB_OGN, CB_RR, CB_RA = 0, 1, 2, 3, 4, 5
CB_MASK0 = 6


def seg_chunks(L):
    out = []
    p = 0
    while p < L:
        n = min(128, L - p)
        out.append((p, n))
        p += n
    return out


def seg_tiles(L):
    ch = seg_chunks(L)
    tiles = []
    i = 0
    while i < len(ch):
        if ch[i][1] < 128:
            tiles.append([i])
            i += 1
        else:
            grp = []
            while i < len(ch) and ch[i][1] == 128 and len(grp) < 4:
                grp.append(i)
                i += 1
            tiles.append(grp)
    return tiles


class Cfg:
    def __init__(self, LP=2064, LS=4100):
        self.LP = LP
        self.LS = LS
        self.TOK = LP + LS
        self.CH_P = seg_chunks(LP)
        self.CH_S = seg_chunks(LS)
        self.NCH_P = len(self.CH_P)
        self.NCH_S = len(self.CH_S)
        rp = LP % 128
        rs = LS % 128
        assert rp > 0 and rs > 0
        self.LENS = (128, rp, rs)
        self.MASK_OFFS = (-128, 128, 128 + rs, rs)
        pc = {}
        n = [0]

        def reg(name, w):
            pc[name] = (n[0], w)
            n[0] += w

        reg("diffF", 128); reg("maskF", 128); reg("diffB", 128); reg("maskB", 128); reg("rowI1", 128)
        for li in range(3):
            reg("rowLen%d" % li, 128)
        for li in range(3):
            reg("colF%d" % li, 1)
        reg("colB", 1)
        reg("lens", 3)
        reg("distW", self.NCH_P + self.NCH_S)
        for nm in ("distCF", "validCF", "distCB", "validCB", "Lsel", "Rsel"):
            reg(nm, 8)
        reg("gn1", DEPTH * 8); reg("gnm", DEPTH * 8); reg("gn2", DEPTH * 8); reg("gfin", 8)
        reg("gng", DEPTH * 4); reg("cw", DEPTH * 12); reg("sink", DEPTH * 4); reg("ldec", DEPTH * 8)
        reg("identf", 128)
        self.pc = pc
        self.NPRM = n[0]
        self.NCB = (6 + len(self.MASK_OFFS)) * 128


def host_consts(cfg, core, inp):
    P = np.zeros((128, cfg.NPRM), np.float32)

    def put(name, arr):
        o, w = cfg.pc[name]
        P[:, o:o + w] = arr

    j = np.arange(128, dtype=np.float32)[:, None]
    i = np.arange(128, dtype=np.float32)[None, :]
    put("diffF", np.maximum(i - j, 0))
    put("maskF", (i >= j).astype(np.float32))
    put("diffB", np.maximum(j - i, 0))
    put("maskB", (j > i).astype(np.float32))
    put("rowI1", np.broadcast_to(i + 1, (128, 128)))
    for li, L_ in enumerate(cfg.LENS):
        put("rowLen%d" % li, np.broadcast_to(L_ - i, (128, 128)))
        put("colF%d" % li, L_ - 1 - j)
    put("colB", j)
    put("lens", np.broadcast_to(np.array(cfg.LENS, np.float32)[None], (128, 3)))
    dw = [cfg.LP - (p + n) for (p, n) in cfg.CH_P] + [cfg.LS - (p + n) for (p, n) in cfg.CH_S]
    put("distW", np.broadcast_to(np.array(dw, np.float32)[None], (128, len(dw))))
    s = core % 4
    grp = core // 4
    dcf = np.zeros(8, np.float32); vcf = np.zeros(8, np.float32)
    dcb = np.zeros(8, np.float32); vcb = np.zeros(8, np.float32)
    ls = np.zeros(8, np.float32); rs = np.zeros(8, np.float32)
    for r in range(8):
        if r // 4 != grp:
            continue
        rq = r % 4
        if rq < s:
            dcf[r] = (s - rq - 1) * cfg.LS
            vcf[r] = 1.0
        if rq > s:
            dcb[r] = (rq - s - 1) * cfg.LS
            vcb[r] = 1.0
        if rq == s - 1:
            ls[r] = 1.0
        if rq == s + 1:
            rs[r] = 1.0
    for nm, a in (("distCF", dcf), ("validCF", vcf), ("distCB", dcb), ("validCB", vcb), ("Lsel", ls), ("Rsel", rs)):
        put(nm, np.broadcast_to(a[None], (128, 8)))

    def gain_lay(g):
        L_ = g.shape[0]
        return g.reshape(L_, 8, 128).transpose(2, 0, 1).reshape(128, L_ * 8)

    put("gn1", gain_lay(inp["norm_ffn1"]))
    put("gnm", gain_lay(inp["norm_mix"]))
    put("gn2", gain_lay(inp["norm_ffn2"]))
    put("gfin", gain_lay(inp["final_norm"][None]))
    put("gng", inp["ret_gn_gain"].reshape(DEPTH, 4, 128).transpose(2, 0, 1).reshape(128, DEPTH * 4))
    put("cw", inp["conv_w"].reshape(DEPTH, 3, 4, 128).transpose(3, 0, 1, 2).reshape(128, DEPTH * 12))
    sk = inp["attn_sink"]
    sl = np.zeros((128, DEPTH, 4), np.float32)
    sl[:64] = sk[None, :, 0:4]
    sl[64:] = sk[None, :, 4:8]
    put("sink", sl.reshape(128, DEPTH * 4))
    put("ldec", np.broadcast_to(inp["ret_decay"].reshape(1, DEPTH * 8), (128, DEPTH * 8)))
    put("identf", np.eye(128, dtype=np.float32))

    B = np.zeros((128, cfg.NCB), np.float32)

    def putb(idx, arr):
        B[:, idx * 128:(idx + 1) * 128] = arr

    putb(CB_IDENT, np.eye(128))
    putb(CB_ONES, np.ones((128, 128)))
    putb(CB_OMEAN, np.full((128, 128), 1.0 / 1024))
    putb(CB_OGN, np.full((128, 128), 1.0 / 128))
    k = np.arange(128)
    Rr = np.zeros((128, 128)); Ra = np.zeros((128, 128))
    Rr[(k + 64) % 128, k] = 1.0
    pa = np.where((k % 64) < 32, k + 32, k - 32)
    Ra[pa, k] = 1.0
    putb(CB_RR, Rr)
    putb(CB_RA, Ra)
    jj = np.arange(128)[:, None]
    ii = np.arange(128)[None, :]
    for mi, off in enumerate(cfg.MASK_OFFS):
        putb(CB_MASK0 + mi, (np.abs(off + jj - ii) <= 128).astype(np.float32))
    return P, B.astype(ml_dtypes.bfloat16)


def host_rotary(cfg, core):
    s = core % 4
    pos = np.concatenate([np.arange(cfg.LP), np.arange(cfg.LS) + s * cfg.LS]).astype(np.float32)

    def tab(d):
        inv = (np.float32(10000.0) ** (-np.arange(0, d, 2, dtype=np.float32) / np.float32(d))).astype(np.float32)
        ang = (pos[:, None] * inv[None, :]).astype(np.float32)
        c = np.cos(ang).astype(np.float32)
        sn = np.sin(ang).astype(np.float32)
        cc = np.concatenate([c, c], 1)
        ss = np.concatenate([-sn, sn], 1)
        reps = 128 // d
        return (np.ascontiguousarray(np.tile(cc, (1, reps)).T), np.ascontiguousarray(np.tile(ss, (1, reps)).T))

    cr, sr = tab(128)
    ca, sa = tab(64)
    return np.stack([cr, sr, ca, sa], 0)


class Buf:
    __slots__ = ("name", "w", "r", "excl")

    def __init__(self, name, excl=False):
        self.name = name
        self.w = None
        self.r = {}
        self.excl = excl


class DSem:
    def __init__(self, h, key):
        self.h = h
        self.key = key
        self.v = 0


class Tracker:
    def __init__(self, nc):
        self.nc = nc
        self.E = {"pe": nc.tensor, "act": nc.scalar, "dve": nc.vector, "pool": nc.gpsimd, "sp": nc.sync}
        self.sems = {}
        self.cnt = {}
        for e in ("pe", "act", "dve", "pool"):
            self.sems[e] = nc.alloc_semaphore(name="sem_" + e)
            self.cnt[e] = 0
        self.waited = {e: {} for e in self.E}
        self.latest = {}
        self.nobarrier = set()
        self.nd = 0

    def dsem(self, name):
        k = "d%d_%s" % (self.nd, name)
        self.nd += 1
        h = self.nc.alloc_semaphore(name=k)
        self.sems[k] = h
        return DSem(h, k)

    def _deps(self, eng, reads, writes):
        deps = {}

        def add(k, v):
            if k == "pe" and eng == "pe":
                return
            if deps.get(k, 0) < v:
                deps[k] = v

        for b in reads:
            if b.w is not None:
                add(*b.w)
            if b.excl:
                for k, v in b.r.items():
                    if k != eng:
                        add(k, v)
        for b in writes:
            if b.w is not None:
                add(*b.w)
            for k, v in b.r.items():
                add(k, v)
        return deps

    def _wait(self, eng, deps):
        w = self.waited[eng]
        for k, v in deps.items():
            if w.get(k, 0) >= v:
                continue
            self.E[eng].wait_ge(self.sems[k], v)
            w[k] = v

    def _record(self, ev, reads, writes):
        k, v = ev
        for b in reads:
            if b.r.get(k, 0) < v:
                b.r[k] = v
        for b in writes:
            b.w = ev
            b.r = {}
        self.latest[k] = v

    def op(self, eng, fn, reads=(), writes=()):
        self._wait(eng, self._deps(eng, reads, writes))
        ins = fn()
        self.cnt[eng] += 1
        ins.then_inc(self.sems[eng], 1)
        self._record((eng, self.cnt[eng]), reads, writes)

    def mmgroup(self, fns, reads, writes):
        self._wait("pe", self._deps("pe", reads, writes))
        ins = None
        for f in fns:
            ins = f()
        self.cnt["pe"] += 1
        ins.then_inc(self.sems["pe"], 1)
        self._record(("pe", self.cnt["pe"]), reads, writes)

    def dma(self, q, fn, ds, reads=(), writes=()):
        self._wait(q, self._deps(q, reads, writes))
        ins = fn()
        ds.v += 16
        ins.then_inc(ds.h, 16)
        self._record((ds.key, ds.v), reads, writes)

    def collective(self, fn, ds, reads=(), writes=()):
        self._wait("pool", self._deps("pool", reads, writes))
        ins = fn()
        ds.v += 1
        ins.then_inc(ds.h)
        self._record((ds.key, ds.v), reads, writes)

    def barrier(self):
        deps = {k: v for k, v in self.latest.items() if k not in self.nobarrier}
        for e in self.E:
            self._wait(e, dict(deps))


class PsPool:
    def __init__(self, slots):
        self.slots = slots
        self.i = 0

    def get(self):
        s = self.slots[self.i]
        self.i = (self.i + 1) % len(self.slots)
        return s


class Slots:
    def __init__(self, prog, n, shape, dt, name):
        self.s = [(prog.sb(shape, dt, name), Buf("%s%d" % (name, i))) for i in range(n)]
        self.i = 0

    def get(self):
        s = self.s[self.i]
        self.i = (self.i + 1) % len(self.s)
        return s


class Prog:
    def __init__(self, cfg=None, nlayers=DEPTH, do_ffn=True, do_mixer=True, do_cc=True, debug=None,
                 mix_parts=("ret", "conv", "att")):
        self.cfg = cfg or Cfg()
        self.debug = debug
        self.nlayers = nlayers
        self.do_ffn = do_ffn
        self.do_mixer = do_mixer
        self.do_cc = do_cc
        self.mix_parts = mix_parts
        nc = bass.Bass("TRN2", target_bir_lowering=False)
        self.nc = nc
        self.T = Tracker(nc)
        self.sb_off = SB_BASE
        self.sb_max = SB_BASE
        self._nm = 0
        self.declare_io()
        self.build()

    def sb(self, shape, dt, name=None):
        nbytes = int(np.prod(shape[1:])) * (4 if dt == F32 else 2)
        nbytes = (nbytes + 31) // 32 * 32
        self._nm += 1
        t = self.nc.alloc_sbuf_tensor_at("%s_%d" % (name or "t", self._nm), list(shape), dt, offset=self.sb_off)
        self.sb_off += nbytes
        self.sb_max = max(self.sb_max, self.sb_off)
        assert self.sb_off <= SB_LIMIT, ("SBUF overflow", name, self.sb_off)
        return t

    def declare_io(self):
        nc = self.nc
        cfg = self.cfg
        TOK = cfg.TOK
        di = lambda n, s, dt=F32: nc.dram_tensor(n, list(s), dt, kind="ExternalInput").ap()
        self.x = di("x", [TOK, D])
        self.prm_d = di("prm", [128, cfg.NPRM])
        self.cb_d = di("cstb", [128, cfg.NCB], BF16)
        self.rot_d = di("rot", [4, 128, TOK])
        self.w = {}
        for nm, shp in (("w_ffn1_gate", [DEPTH, D, DFF]), ("w_ffn1_up", [DEPTH, D, DFF]), ("w_ffn1_down", [DEPTH, DFF, D]),
                        ("w_in", [DEPTH, D, INW]), ("w_ret_out", [DEPTH, 512, D]), ("w_conv_out", [DEPTH, 512, D]),
                        ("w_attn_out", [DEPTH, 512, D]), ("w_o", [DEPTH, D, D]),
                        ("w_ffn2_gate", [DEPTH, D, DFF]), ("w_ffn2_up", [DEPTH, D, DFF]), ("w_ffn2_down", [DEPTH, DFF, D])):
            self.w[nm] = di(nm, shp)
        self.y = nc.dram_tensor("y", [TOK, D], F32, kind="ExternalOutput").ap()
        self.hT = nc.dram_tensor("hT", [D, TOK], F32).ap()
        self.hTv = self.hT.rearrange("(k p) t -> p k t", p=128)
        self.wb = {}
        self.wtot = 0
        for l in range(DEPTH):
            for f in (1, 2):
                for j in range(NJ):
                    self.wreg(("gu", l, f, j), 8 * 256)
                for m in range(8):
                    self.wreg(("dn", l, f, m), NJ * 128)
            self.wreg(("pak", l), 8 * 512)
            self.wreg(("pav", l), 8 * 512)
            self.wreg(("paa", l), 8 * 256)
            for h in range(4):
                self.wreg(("rh", l, h), 8 * 512)
            for c in ("cc", "cx", "cb"):
                self.wreg((c, l), 8 * 512)
            self.wreg(("aq", l), 8 * 512)
            for m in range(8):
                self.wreg(("gg", l, m), 8 * 384)
                self.wreg(("go", l, m), 12 * 128)
            for hf in range(2):
                self.wreg(("wo", l, hf), 8 * 512)
        self.wscr = nc.dram_tensor("wscr", [128, self.wtot], BF16).ap()
        self.sbl_d = nc.dram_tensor("sbl", [cfg.NCH_P + cfg.NCH_S, 128, 512], BF16).ap()
        self.sendF = nc.dram_tensor("sendF", [128, 1024], F32)
        self.gathF = nc.dram_tensor("gathF", [NRANK * 128, 1024], F32)
        self.sendB = nc.dram_tensor("sendB", [128, 528], BF16)
        self.gathB = nc.dram_tensor("gathB", [NRANK * 128, 528], BF16)

    def wreg(self, key, width):
        self.wb[key] = (self.wtot, width)
        self.wtot += width

    def wview(self, key):
        o, w = self.wb[key]
        return self.wscr[:, o:o + w]

    def convert_layer(self, l):
        T = self.T
        nc = self.nc
        dss = {}
        for g in ("f1", "mx", "f2"):
            dss[g] = T.dsem("cvt%d%s" % (l, g))
            T.nobarrier.add(dss[g].key)
        cur = ["f1"]

        def cv(dst, src):
            g = cur[0]
            T.dma("pool", lambda: nc.gpsimd.dma_start(out=dst, in_=src), dss[g], reads=(), writes=(self.cvt_buf[(l, g)],))

        def kview(key, k):
            return self.wview(key).rearrange("p (k c) -> p k c", k=k)

        def conv_ffn(f):
            if True:
                cur[0] = "f%d" % f
                wg = self.w["w_ffn%d_gate" % f][l].rearrange("(k p) c -> p k c", p=128)
                wu = self.w["w_ffn%d_up" % f][l].rearrange("(k p) c -> p k c", p=128)
                wd = self.w["w_ffn%d_down" % f][l].rearrange("(j p) c -> p j c", p=128)
                for j in range(NJ):
                    v = kview(("gu", l, f, j), 8)
                    cv(v[:, :, 0:128], wg[:, :, j * 128:(j + 1) * 128])
                    cv(v[:, :, 128:256], wu[:, :, j * 128:(j + 1) * 128])
                for m in range(8):
                    v = kview(("dn", l, f, m), NJ)
                    cv(v[:, :, :], wd[:, :, m * 128:(m + 1) * 128])
        if self.do_ffn:
            conv_ffn(1)
        if self.do_mixer:
            cur[0] = "mx"
            self.convert_mixer(l, cv, kview)
        if self.do_ffn:
            conv_ffn(2)

    def convert_mixer(self, l, cv, kview):
        win = self.w["w_in"][l].rearrange("(k p) c -> p k c", p=128)
        RQ, RK, RV, RG, CBo, CCo, CXo, AQ, AK, AV, GT = 0, 512, 1024, 1536, 2048, 2560, 3072, 3584, 4096, 4224, 4352
        cv(kview(("pak", l), 8)[:, :, :], win[:, :, RK:RK + 512])
        cv(kview(("pav", l), 8)[:, :, :], win[:, :, RV:RV + 512])
        v = kview(("paa", l), 8)
        cv(v[:, :, 0:128], win[:, :, AK:AK + 128])
        cv(v[:, :, 128:256], win[:, :, AV:AV + 128])
        for h in range(4):
            v = kview(("rh", l, h), 8)
            for pi, base in enumerate((RQ, RK, RV, RG)):
                cv(v[:, :, pi * 128:(pi + 1) * 128], win[:, :, base + h * 128: base + (h + 1) * 128])
        cv(kview(("cc", l), 8)[:, :, :], win[:, :, CCo:CCo + 512])
        cv(kview(("cx", l), 8)[:, :, :], win[:, :, CXo:CXo + 512])
        cv(kview(("cb", l), 8)[:, :, :], win[:, :, CBo:CBo + 512])
        v = kview(("aq", l), 8)
        for j in range(4):
            cv(v[:, :, j * 128: j * 128 + 64], win[:, :, AQ + j * 64: AQ + (j + 1) * 64])
            cv(v[:, :, j * 128 + 64: (j + 1) * 128], win[:, :, AQ + (4 + j) * 64: AQ + (5 + j) * 64])
        wro = self.w["w_ret_out"][l].rearrange("(k p) c -> p k c", p=128)
        wco = self.w["w_conv_out"][l].rearrange("(k p) c -> p k c", p=128)
        wao = self.w["w_attn_out"][l]
        for m in range(8):
            gv = kview(("gg", l, m), 8)
            for b in range(3):
                cv(gv[:, :, b * 128:(b + 1) * 128], win[:, :, GT + b * 1024 + m * 128: GT + b * 1024 + (m + 1) * 128])
            ov = kview(("go", l, m), 12)
            cv(ov[:, 0:4, :], wro[:, :, m * 128:(m + 1) * 128])
            cv(ov[:, 4:8, :], wco[:, :, m * 128:(m + 1) * 128])
            for j in range(4):
                cv(ov[0:64, 8 + j, :], wao[j * 64:(j + 1) * 64, m * 128:(m + 1) * 128])
                cv(ov[64:128, 8 + j, :], wao[(4 + j) * 64:(5 + j) * 64, m * 128:(m + 1) * 128])
        wo = self.w["w_o"][l].rearrange("(k p) c -> p k c", p=128)
        for hf in range(2):
            cv(kview(("wo", l, hf), 8)[:, :, :], wo[:, :, hf * 512:(hf + 1) * 512])

    def ring_init(self, nslots, width):
        self.ring = []
        for i in range(nslots):
            t = self.sb([128, width], BF16, "ring")
            self.ring.append((t, Buf("ring%d" % i), self.T.dsem("ring%d" % i)))
        self.ring_issue = 0
        self.ring_use = 0
        self.ring_seq = []

    def ring_prefetch(self, keep=0):
        T = self.T
        nc = self.nc
        while self.ring_issue < len(self.ring_seq) and self.ring_issue < self.ring_use - keep + len(self.ring):
            key = self.ring_seq[self.ring_issue]
            t, b, ds = self.ring[self.ring_issue % len(self.ring)]
            o, w = self.wb[key]
            l = key[1]
            g = ("f%d" % key[2]) if key[0] in ("gu", "dn") else "mx"
            src = self.wscr[:, o:o + w]
            T.dma("sp", lambda: nc.sync.dma_start(out=t[:, 0:w], in_=src), ds, reads=(self.cvt_buf[(l, g)],), writes=(b,))
            self.ring_issue += 1

    def ring_next(self, key, keep=0):
        assert self.ring_seq[self.ring_use] == key, (self.ring_seq[self.ring_use], key)
        self.ring_prefetch(keep)
        t, b, ds = self.ring[self.ring_use % len(self.ring)]
        self.ring_use += 1
        return t, b

    def weight_sequence(self):
        seq = []
        for l in range(self.nlayers):
            if self.do_ffn:
                for g in self.ffn_groups:
                    seq += [("gu", l, 1, j) for j in range(NJ)]
                    seq += [("dn", l, 1, m) for m in range(8)]
            if self.do_mixer:
                for seg in self.mixer_order:
                    for tl in reversed(self.tiles[seg]):
                        seq += [("pak", l), ("pav", l), ("paa", l)]
                    for tl in self.tiles[seg]:
                        if "ret" in self.mix_parts:
                            seq += [("rh", l, h) for h in range(4)]
                        if "conv" in self.mix_parts:
                            seq += [("cc", l), ("cx", l), ("cb", l)]
                        if "att" in self.mix_parts:
                            seq += [("aq", l)]
                        for m in range(8):
                            seq += [("gg", l, m), ("go", l, m)]
                        seq += [("wo", l, 0), ("wo", l, 1)]
            if self.do_ffn:
                for g in self.ffn_groups:
                    seq += [("gu", l, 2, j) for j in range(NJ)]
                    seq += [("dn", l, 2, m) for m in range(8)]
        return seq

    def act(self, out, in_, func, reads, writes, **kw):
        nc = self.nc
        self.T.op("act", lambda: nc.scalar.activation(out=out, in_=in_, func=func, **kw), reads, writes)

    def tt(self, eng, out, in0, in1, op, reads, writes):
        e = self.nc.vector if eng == "dve" else self.nc.gpsimd
        self.T.op(eng, lambda: e.tensor_tensor(out=out, in0=in0, in1=in1, op=op), reads, writes)

    def ts(self, eng, out, in0, s1, op0, reads, writes, s2=None, op1=None):
        e = self.nc.vector if eng == "dve" else self.nc.gpsimd
        if s2 is None:
            self.T.op(eng, lambda: e.tensor_scalar(out=out, in0=in0, scalar1=s1, scalar2=None, op0=op0), reads, writes)
        else:
            self.T.op(eng, lambda: e.tensor_scalar(out=out, in0=in0, scalar1=s1, scalar2=s2, op0=op0, op1=op1), reads, writes)

    def stt(self, out, in0, scalar, in1, op0, op1, reads, writes):
        nc = self.nc
        self.T.op("dve", lambda: nc.vector.scalar_tensor_tensor(out=out, in0=in0, scalar=scalar, in1=in1, op0=op0, op1=op1), reads, writes)

    def copy(self, eng, out, in_, reads, writes):
        nc = self.nc
        if eng == "act":
            self.act(out, in_, AF.Copy, reads, writes)
        elif eng == "dve":
            self.T.op("dve", lambda: nc.vector.tensor_copy(out=out, in_=in_), reads, writes)
        else:
            self.T.op("pool", lambda: nc.gpsimd.tensor_copy(out=out, in_=in_), reads, writes)

    def mm(self, out, pairs, reads, writes):
        nc = self.nc
        n = len(pairs)
        fns = []
        for i, (l, r) in enumerate(pairs):
            fns.append(lambda l=l, r=r, i=i: nc.tensor.matmul(out, lhsT=l, rhs=r, start=(i == 0), stop=(i == n - 1)))
        self.T.mmgroup(fns, reads, writes)

    def mm_multi(self, groups, reads, writes):
        nc = self.nc
        fns = []
        for out, pairs in groups:
            n = len(pairs)
            for i, (l, r) in enumerate(pairs):
                fns.append(lambda out=out, l=l, r=r, i=i, n=n: nc.tensor.matmul(out, lhsT=l, rhs=r, start=(i == 0), stop=(i == n - 1)))
        self.T.mmgroup(fns, reads, writes)

    def transposes(self, items, reads, writes):
        nc = self.nc
        fns = [(lambda o=o, i=i, d=d: nc.tensor.transpose(o, i, d)) for (o, i, d) in items]
        self.T.mmgroup(fns, reads, writes)

    def prm(self, name, a=0, b=None):
        o, w = self.cfg.pc[name]
        if b is None:
            b = w
        return self.prm_t[:, o + a:o + b]

    def cbv(self, idx, rows=128, cols=128):
        return self.cb_t[0:rows, idx * 128: idx * 128 + cols]

    def build(self):
        nc = self.nc
        T = self.T
        cfg = self.cfg
        bigs = [nc.alloc_psum_tensor("big%d" % i, [128, 512], F32) for i in range(6)]
        bfs = [nc.alloc_psum_tensor("bfp%d" % i, [128, 1024], BF16) for i in range(2)]
        self.big = PsPool([(t, Buf("big%d" % i, excl=True)) for i, t in enumerate(bigs)])
        self.bfp = PsPool([(t, Buf("bfp%d" % i, excl=True)) for i, t in enumerate(bfs)])

        self.prm_t = self.sb([128, cfg.NPRM], F32, "prm")
        self.cb_t = self.sb([128, cfg.NCB], BF16, "cb")
        self.epsc = self.sb([128, 1], F32, "eps")
        self.lg = self.sb([128, DEPTH * 8], F32, "lg")
        self.esink = self.sb([128, DEPTH * 4], F32, "esink")
        Bc = Buf("consts")
        self.Bc = Bc
        dsc = T.dsem("const")
        T.dma("sp", lambda: nc.sync.dma_start(out=self.prm_t[:], in_=self.prm_d[:, :]), dsc, (), (Bc,))
        T.dma("sp", lambda: nc.sync.dma_start(out=self.cb_t[:], in_=self.cb_d[:, :]), dsc, (), (Bc,))
        T.op("dve", lambda: nc.vector.memset(self.epsc[:], EPS), (), (Bc,))
        self.act(self.lg[:], self.prm("ldec"), AF.Exp, (Bc,), (Bc,))
        self.act(self.lg[:], self.lg[:], AF.Ln, (Bc,), (Bc,), scale=-1.0, bias=1.0)
        self.act(self.esink[:], self.prm("sink"), AF.Exp, (Bc,), (Bc,))

        self.seg_off = {"P": 0, "S": cfg.LP}
        self.seg_len = {"P": cfg.LP, "S": cfg.LS}
        self.chunks = {"P": cfg.CH_P, "S": cfg.CH_S}
        self.chbase = {"P": 0, "S": cfg.NCH_P}
        self.tiles = {"P": seg_tiles(cfg.LP), "S": seg_tiles(cfg.LS)}
        self.mixer_order = ["P", "S"]
        self.GT = 1376
        ffn_tiles = []
        for seg in ("P", "S"):
            L = self.seg_len[seg]
            nt = (L + 511) // 512
            base = L // nt
            rem = L - base * nt
            p = self.seg_off[seg]
            for i in range(nt):
                n = base + (1 if i < rem else 0)
                ffn_tiles.append((p, n))
                p += n
        groups = []
        cur = []
        tot = 0
        for (p, n) in ffn_tiles:
            if tot + n > self.GT:
                groups.append(cur)
                cur = []
                tot = 0
            cur.append((p, n))
            tot += n
        if cur:
            groups.append(cur)
        self.ffn_groups = groups

        self.cvt_buf = {(l, g): Buf("cvt%d%s" % (l, g)) for l in range(DEPTH) for g in ("f1", "mx", "f2")}
        if self.debug not in ("none", "init", "init1") and self.nlayers > 0:
            self.convert_layer(0)

        self.ring_init(3, 4096)
        self.ring_seq = self.weight_sequence()

        self.hp = []
        for i in range(3):
            self.hp.append((self.sb([128, 512], F32, "hp"), Buf("hp%d" % i), T.dsem("hpl%d" % i), T.dsem("hps%d" % i)))
        self.hp_i = 0
        self.Bh = Buf("hT")
        self.base_off = self.sb_off

        if self.debug in ("none", "cvt"):
            T.barrier()
            return
        self.init_transpose()
        if self.debug in ("init", "init1"):
            T.barrier()
            return
        for l in range(self.nlayers):
            if self.do_ffn:
                self.ffn(l, 1)
            if self.do_mixer:
                self.mixer(l)
            if l + 1 < self.nlayers and not self.do_ffn:
                self.convert_layer(l + 1)
            if self.do_ffn:
                self.ffn(l, 2, cvt_next=(l + 1 if l + 1 < self.nlayers else None))
        self.final_out()
        T.barrier()

    def init_transpose(self):
        nc = self.nc
        T = self.T
        TOK = self.cfg.TOK
        self.sb_off = self.base_off
        xt = [(self.sb([128, D], F32, "xt"), Buf("xt%d" % i), T.dsem("xt%d" % i)) for i in range(2)]
        st = [(self.sb([128, 8, 128], F32, "xst"), Buf("xst%d" % i), T.dsem("xst%d" % i)) for i in range(2)]
        identf = self.prm("identf")
        nblk = (TOK + 127) // 128
        if self.debug == "init1":
            nblk = 1
        for tb in range(nblk):
            t0 = tb * 128
            n = min(128, TOK - t0)
            xtt, xb, xds = xt[tb % 2]
            stt_, sbuf, sds = st[tb % 2]
            T.dma("sp", lambda: nc.sync.dma_start(out=xtt[0:n, :], in_=self.x[t0:t0 + n, :]), xds, (), (xb,))
            for half in range(2):
                pt, pb = self.big.get()
                self.transposes([(pt[:, q * 128: q * 128 + n], xtt[0:n, (half * 4 + q) * 128:(half * 4 + q + 1) * 128], identf[0:n, 0:n])
                                 for q in range(4)], (xb, self.Bc), (pb,))
                src = pt[:, :].rearrange("p (q c) -> p q c", q=4)[:, :, 0:n]
                self.copy("dve" if half == 0 else "act", stt_[:, half * 4:half * 4 + 4, 0:n], src, (pb,), (sbuf,))
            T.dma("sp", lambda: nc.sync.dma_start(out=self.hTv[:, :, t0:t0 + n], in_=stt_[:, :, 0:n]), sds, (sbuf,), (self.Bh,))
        T.barrier()

    def norm_tile(self, t0, n, gain, xn_ap_fn, xn_buf, tmp):
        nc = self.nc
        T = self.T
        sq, lnv, lnb = tmp
        pt, pb = self.big.get()
        for k in range(KC):
            hp, hb, hl, hs = self.hp[self.hp_i % 3]
            self.hp_i += 1
            T.dma("sp", lambda: nc.sync.dma_start(out=hp[:, 0:n], in_=self.hTv[:, k, t0:t0 + n]), hl, (self.Bh,), (hb,))
            s, sb_ = sq.get()
            self.act(s[:, 0:n], hp[:, 0:n], AF.Square, (hb,), (sb_,))
            T.mmgroup([lambda s=s, k=k: nc.tensor.matmul(pt[:, 0:n], lhsT=self.cbv(CB_OMEAN), rhs=s[:, 0:n], start=(k == 0), stop=(k == KC - 1))],
                      (sb_, self.Bc), (pb,))
        self.act(lnv[:, 0:n], pt[:, 0:n], AF.Ln, (pb, self.Bc), (lnb,), bias=self.epsc[:, 0:1])
        self.act(lnv[:, 0:n], lnv[:, 0:n], AF.Exp, (lnb,), (lnb,), scale=-0.5)
        for k in range(KC):
            hp, hb, hl, hs = self.hp[self.hp_i % 3]
            self.hp_i += 1
            T.dma("sp", lambda: nc.sync.dma_start(out=hp[:, 0:n], in_=self.hTv[:, k, t0:t0 + n]), hl, (self.Bh,), (hb,))
            self.stt(xn_ap_fn(k), hp[:, 0:n], gain[:, k:k + 1], lnv[:, 0:n], ALU.mult, ALU.mult, (hb, lnb, self.Bc), (xn_buf,))

    def resid_add(self, m, t0, n, ps_ap, ps_buf, scale):
        nc = self.nc
        T = self.T
        hp, hb, hl, hs = self.hp[self.hp_i % 3]
        self.hp_i += 1
        T.dma("sp", lambda: nc.sync.dma_start(out=hp[:, 0:n], in_=self.hTv[:, m, t0:t0 + n]), hl, (self.Bh,), (hb,))
        self.stt(hp[:, 0:n], ps_ap, scale, hp[:, 0:n], ALU.mult, ALU.add, (ps_buf, hb), (hb,))
        T.dma("sp", lambda: nc.sync.dma_start(out=self.hTv[:, m, t0:t0 + n], in_=hp[:, 0:n]), hs, (hb,), (self.Bh,))

    def ffn(self, l, f, cvt_next=None):
        T = self.T
        T.barrier()
        if cvt_next is not None:
            self.convert_layer(cvt_next)
        self.sb_off = self.base_off
        GT = self.GT
        xn = self.sb([128, KC, GT], BF16, "fxn")
        hid = self.sb([128, NJ, GT], BF16, "fhid")
        sq = Slots(self, 2, [128, 512], BF16, "fsq")
        lnv = self.sb([128, 512], F32, "flnv")
        lnb = Buf("flnv")
        sg = Slots(self, 2, [128, 512], F32, "fsg")
        gain = self.prm("gn1" if f == 1 else "gn2", l * 8, l * 8 + 8)
        for grp in self.ffn_groups:
            xnb = [Buf("fxn%d" % i) for i in range(len(grp))]
            hidb = [[Buf("fh%d_%d" % (j, i)) for i in range(len(grp))] for j in range(NJ)]
            offs = []
            o = 0
            for (t0, n) in grp:
                offs.append(o)
                o += n
            for ti, (t0, n) in enumerate(grp):
                self.norm_tile(t0, n, gain, lambda k, ti=ti, n=n: xn[:, k, offs[ti]:offs[ti] + n], xnb[ti], (sq, lnv, lnb))
            for j in range(NJ):
                wt, wbuf = self.ring_next(("gu", l, f, j))
                wv = wt[:, 0:2048].rearrange("p (k c) -> p k c", k=8)
                for ti, (t0, n) in enumerate(grp):
                    pg, pgb = self.big.get()
                    pu, pub = self.big.get()
                    self.mm(pg[:, 0:n], [(wv[:, k, 0:128], xn[:, k, offs[ti]:offs[ti] + n]) for k in range(KC)], (wbuf, xnb[ti]), (pgb,))
                    self.mm(pu[:, 0:n], [(wv[:, k, 128:256], xn[:, k, offs[ti]:offs[ti] + n]) for k in range(KC)], (wbuf, xnb[ti]), (pub,))
                    s, sb_ = sg.get()
                    self.act(s[:, 0:n], pg[:, 0:n], AF.Silu, (pgb,), (sb_,))
                    self.tt("dve", hid[:, j, offs[ti]:offs[ti] + n], pu[:, 0:n], s[:, 0:n], ALU.mult, (pub, sb_), (hidb[j][ti],))
            for m in range(8):
                wt, wbuf = self.ring_next(("dn", l, f, m))
                wv = wt[:, 0:NJ * 128].rearrange("p (j c) -> p j c", j=NJ)
                for ti, (t0, n) in enumerate(grp):
                    pd, pdb = self.big.get()
                    self.mm(pd[:, 0:n], [(wv[:, j, :], hid[:, j, offs[ti]:offs[ti] + n]) for j in range(NJ)],
                            [wbuf] + [hidb[j][ti] for j in range(NJ)], (pdb,))
                    self.resid_add(m, t0, n, pd[:, 0:n], pdb, 0.5)

    def final_out(self):
        nc = self.nc
        T = self.T
        TOK = self.cfg.TOK
        T.barrier()
        self.sb_off = self.base_off
        xn = self.sb([128, KC, 512], F32, "oxn")
        sq = Slots(self, 2, [128, 512], BF16, "osq")
        lnv = self.sb([128, 512], F32, "olnv")
        lnb = Buf("olnv")
        ot = [(self.sb([128, D], F32, "ot"), Buf("ot%d" % i), T.dsem("ot%d" % i)) for i in range(2)]
        identf = self.prm("identf")
        gain = self.prm("gfin")
        oi = 0
        t0 = 0
        while t0 < TOK:
            n = min(512, TOK - t0)
            xb = Buf("oxn")
            self.norm_tile(t0, n, gain, lambda k, n=n: xn[:, k, 0:n], xb, (sq, lnv, lnb))
            c0 = 0
            while c0 < n:
                cn = min(128, n - c0)
                ott, ob, ods = ot[oi % 2]
                oi += 1
                for half in range(2):
                    pt, pb = self.big.get()
                    self.transposes([(pt[0:cn, q * 128:(q + 1) * 128], xn[:, half * 4 + q, c0:c0 + cn], identf) for q in range(4)],
                                    (xb, self.Bc), (pb,))
                    self.copy("dve" if half == 0 else "act", ott[0:cn, half * 512:(half + 1) * 512], pt[0:cn, :], (pb,), (ob,))
                T.dma("sp", lambda: nc.sync.dma_start(out=self.y[t0 + c0:t0 + c0 + cn, :], in_=ott[0:cn, :]), ods, (ob,), ())
                c0 += cn
            t0 += n

    def layer_consts(self, l):
        m = self.M
        B = m["Bd"]
        tA, tAb = m["tmpA"]
        tB, tBb = m["tmpB"]
        s128 = float(128 ** -0.5)
        for h in range(4):
            lf = self.lg[:, l * 8 + h: l * 8 + h + 1]
            lb = self.lg[:, l * 8 + 4 + h: l * 8 + 4 + h + 1]
            self.act(tA[:, 0:128], self.prm("diffF"), AF.Exp, (self.Bc,), (tAb,), scale=lf)
            self.tt("dve", tA[:, 0:128], tA[:, 0:128], self.prm("maskF"), ALU.mult, (tAb, self.Bc), (tAb,))
            self.act(tB[:, 0:128], self.prm("diffB"), AF.Exp, (self.Bc,), (tBb,), scale=lb)
            self.tt("dve", tB[:, 0:128], tB[:, 0:128], self.prm("maskB"), ALU.mult, (tBb, self.Bc), (tBb,))
            self.tt("dve", tA[:, 0:128], tA[:, 0:128], tB[:, 0:128], ALU.add, (tAb, tBb), (tAb,))
            self.ts("dve", m["Dc"][:, h, :], tA[:, 0:128], s128, ALU.mult, (tAb,), (B,))
            self.act(m["rowF"][:, h, :], self.prm("rowI1"), AF.Exp, (self.Bc,), (B,), scale=lf)
            for li in range(3):
                self.act(m["rowB"][:, li, h, :], self.prm("rowLen%d" % li), AF.Exp, (self.Bc,), (B,), scale=lb)
                self.act(m["cdF"][:, li, h:h + 1], self.prm("colF%d" % li), AF.Exp, (self.Bc,), (B,), scale=lf)
                self.act(m["cF"][:, li, h:h + 1], self.prm("lens", li, li + 1), AF.Exp, (self.Bc,), (B,), scale=lf)
                self.act(m["cB"][:, li, h:h + 1], self.prm("lens", li, li + 1), AF.Exp, (self.Bc,), (B,), scale=lb)
            self.act(m["cdB"][:, h:h + 1], self.prm("colB"), AF.Exp, (self.Bc,), (B,), scale=lb)
            self.act(m["wF"][:, h, :], self.prm("distW"), AF.Exp, (self.Bc,), (B,), scale=lf)
            self.act(m["wB"][:, h, :], self.prm("distW"), AF.Exp, (self.Bc,), (B,), scale=lb)
            self.act(m["coF"][:, h, :], self.prm("distCF"), AF.Exp, (self.Bc,), (B,), scale=lf)
            self.act(m["coB"][:, h, :], self.prm("distCB"), AF.Exp, (self.Bc,), (B,), scale=lb)
            self.tt("dve", m["coF"][:, h, :], m["coF"][:, h, :], self.prm("validCF"), ALU.mult, (B, self.Bc), (B,))
            self.tt("dve", m["coB"][:, h, :], m["coB"][:, h, :], self.prm("validCB"), ALU.mult, (B, self.Bc), (B,))
        self.ts("dve", m["cdF"][:, :, :], m["cdF"][:, :, :], s128, ALU.mult, (B,), (B,))
        self.ts("dve", m["cdB"][:, :], m["cdB"][:, :], s128, ALU.mult, (B,), (B,))

    def rotary(self, ps, psb, n, kind, dst, dst_buf):
        m = self.M
        rot, rotb = m["rot"], m["rotb"]
        ci = 0 if kind == "r" else 2
        R = self.cbv(CB_RR if kind == "r" else CB_RA)
        qb, qbb = m["rq"].get()
        qc, qcb = m["rqc"].get()
        tt_, ttb = m["rqt"].get()
        self.copy("act", qb[:, 0:n], ps, (psb,), (qbb,))
        self.tt("dve", qc[:, 0:n], ps, rot[:, ci, 0:n], ALU.mult, (psb, rotb), (qcb,))
        pw, pwb = self.big.get()
        self.mm(pw[:, 0:n], [(R, qb[:, 0:n])], (qbb, self.Bc), (pwb,))
        self.tt("dve", tt_[:, 0:n], pw[:, 0:n], rot[:, ci + 1, 0:n], ALU.mult, (pwb, rotb), (ttb,))
        self.tt("pool", dst, tt_[:, 0:n], qc[:, 0:n], ALU.add, (ttb, qcb), (dst_buf,))

    def load_rot(self, t0, n):
        nc = self.nc
        m = self.M
        self.T.dma("sp", lambda: nc.sync.dma_start(out=m["rot"][:, :, 0:n], in_=self.rot_d[:, :, t0:t0 + n].rearrange("a p t -> p a t")),
                   m["rotds"], (), (m["rotb"],))

    def tile_info(self, seg, tl):
        ch = self.chunks[seg]
        cl = [ch[c] for c in tl]
        s0 = cl[0][0]
        n = sum(c[1] for c in cl)
        t0 = self.seg_off[seg] + s0
        locs = [(c[0] - s0, c[1]) for c in cl]
        uniform = all(c[1] == 128 for c in cl)
        return s0, t0, n, locs, uniform

    def len_idx(self, seg, nc_):
        if nc_ == 128:
            return 0
        return 1 if seg == "P" else 2

    def mixer(self, l):
        T = self.T
        cfg = self.cfg
        T.barrier()
        self.sb_off = self.base_off
        NCH = cfg.NCH_P + cfg.NCH_S
        LX = max(cfg.LP, cfg.LS)
        NVB = max(cfg.NCH_P, cfg.NCH_S) + 2
        m = {}
        self.M = m
        m["kaT"] = self.sb([128, LX + 256], BF16, "kaT")
        m["va"] = self.sb([128, NVB, 128], BF16, "va")
        m["kaTb"] = {}
        m["vab"] = {}
        m["xn"] = Slots(self, 2, [128, KC, 512], BF16, "mxn")
        m["xe"] = self.sb([128, 16], BF16, "xe")
        m["xeb"] = Buf("xe")
        m["sq"] = Slots(self, 2, [128, 512], BF16, "msq")
        m["lnv"] = self.sb([128, 512], F32, "mlnv")
        m["lnb"] = Buf("mlnv")
        m["rot"] = self.sb([128, 4, 512], F32, "rot")
        m["rotb"] = Buf("rot")
        m["rotds"] = T.dsem("rot")
        m["Dc"] = self.sb([128, 4, 128], F32, "Dc")
        m["rowF"] = self.sb([128, 4, 128], F32, "rowF")
        m["rowB"] = self.sb([128, 3, 4, 128], F32, "rowB")
        m["cdF"] = self.sb([128, 3, 4], F32, "cdF")
        m["cdB"] = self.sb([128, 4], F32, "cdB")
        m["cF"] = self.sb([128, 3, 4], F32, "cF")
        m["cB"] = self.sb([128, 3, 4], F32, "cB")
        m["wF"] = self.sb([128, 4, NCH], F32, "wF")
        m["wB"] = self.sb([128, 4, NCH], F32, "wB")
        m["coF"] = self.sb([128, 4, 8], F32, "coF")
        m["coB"] = self.sb([128, 4, 8], F32, "coB")
        m["Bd"] = Buf("derived")
        m["Sfin"] = self.sb([128, 4, 128], F32, "Sfin")
        m["Sbin"] = self.sb([128, 4, 128], F32, "Sbin")
        m["Sinb"] = Buf("Sin")
        m["onesLR"] = self.sb([128, 2, 128], BF16, "onesLR")
        m["xhe"] = self.sb([128, 2, 8], BF16, "xhe")
        m["halob"] = Buf("halo")
        m["rq"] = Slots(self, 2, [128, 512], BF16, "rq")
        m["rqc"] = Slots(self, 2, [128, 512], F32, "rqc")
        m["rqt"] = Slots(self, 2, [128, 512], F32, "rqt")
        m["tmpA"] = (self.sb([128, 512], F32, "tmpA"), Buf("tmpA"))
        m["tmpB"] = (self.sb([128, 512], F32, "tmpB"), Buf("tmpB"))
        m["Bsbl"] = {}
        self.mix_common_off = self.sb_off
        self.layer_consts(l)

        for seg in self.mixer_order:
            self.pass_a(l, seg)
            self.pass_b(l, seg)

    def pass_a(self, l, seg):
        nc = self.nc
        T = self.T
        m = self.M
        T.barrier()
        self.sb_off = self.mix_common_off
        L = self.seg_len[seg]
        ch = self.chunks[seg]
        kT = self.sb([128, 4, 512], BF16, "akT")
        kTb = Buf("akT")
        v4 = self.sb([128, 4, 512], BF16, "av4")
        v4b = Buf("av4")
        kdf = self.sb([128, 4, 128], BF16, "akdf")
        kdb = self.sb([128, 4, 128], BF16, "akdb")
        kdfb, kdbb = Buf("akdf"), Buf("akdb")
        SbRun = self.sb([128, 4, 128], F32, "SbRun")
        SfAcc = self.sb([128, 4, 128], F32, "SfAcc")
        Srb, Sab = Buf("SbRun"), Buf("SfAcc")
        stage = Slots(self, 2, [128, 4, 512], BF16, "sblst")
        stds = [T.dsem("sblst%d" % i) for i in range(2)]
        sti = 0
        tA, tAb = m["tmpA"]
        tB, tBb = m["tmpB"]
        Bsbl = m["Bsbl"]
        T.op("pool", lambda: nc.gpsimd.memset(SbRun[:], 0.0), (), (Srb,))
        T.op("pool", lambda: nc.gpsimd.memset(SfAcc[:], 0.0), (), (Sab,))
        gain = self.prm("gnm", l * 8, l * 8 + 8)
        tiles = self.tiles[seg]
        for tl in reversed(tiles):
            s0, t0, n, locs, uniform = self.tile_info(seg, tl)
            xn, xnb = m["xn"].get()
            self.norm_tile(t0, n, gain, lambda k: xn[:, k, 0:n], xnb, (m["sq"], m["lnv"], m["lnb"]))
            if seg == "S":
                if tl is tiles[0]:
                    self.copy("pool", m["xe"][:, 0:8], xn[:, :, 0], (xnb,), (m["xeb"],))
                if tl is tiles[-1]:
                    self.copy("pool", m["xe"][:, 8:16], xn[:, :, n - 1], (xnb,), (m["xeb"],))
            self.load_rot(t0, n)
            wk, wkb = self.ring_next(("pak", l))
            wv_, wvb = self.ring_next(("pav", l), keep=1)
            wa, wab = self.ring_next(("paa", l), keep=2)
            wkv = wk[:, 0:4096].rearrange("p (k c) -> p k c", k=8)
            wvv = wv_[:, 0:4096].rearrange("p (k c) -> p k c", k=8)
            wav = wa[:, 0:2048].rearrange("p (k c) -> p k c", k=8)
            for h in range(4):
                ps, psb = self.big.get()
                self.mm(ps[:, 0:n], [(wkv[:, k, h * 128:(h + 1) * 128], xn[:, k, 0:n]) for k in range(KC)], (wkb, xnb), (psb,))
                self.rotary(ps[:, 0:n], psb, n, "r", kT[:, h, 0:n], kTb)
            ps, psb = self.big.get()
            self.mm(ps[:, 0:n], [(wav[:, k, 0:128], xn[:, k, 0:n]) for k in range(KC)], (wab, xnb), (psb,))
            kab = Buf("kaT_%s_%d" % (seg, s0))
            m["kaTb"][(seg, s0)] = kab
            self.rotary(ps[:, 0:n], psb, n, "a", m["kaT"][:, 128 + s0:128 + s0 + n], kab)
            for ci, (cl, ncc) in enumerate(locs):
                ps, psb = self.big.get()
                self.mm(ps[0:ncc, :], [(xn[:, k, cl:cl + ncc], wvv[:, k, :]) for k in range(KC)], (wvb, xnb), (psb,))
                self.copy("act", v4[0:ncc, ci, :], ps[0:ncc, :], (psb,), (v4b,))
            ps, psb = self.big.get()
            self.mm_multi([(ps[0:ncc, ci * 128:(ci + 1) * 128], [(xn[:, k, cl:cl + ncc], wav[:, k, 128:256]) for k in range(KC)])
                           for ci, (cl, ncc) in enumerate(locs)], (wab, xnb), (psb,))
            for ci, (cl, ncc) in enumerate(locs):
                c = tl[ci]
                vb = Buf("va_%s_%d" % (seg, c))
                m["vab"][(seg, c)] = vb
                self.copy("dve", m["va"][0:ncc, 1 + c, :], ps[0:ncc, ci * 128:(ci + 1) * 128], (psb,), (vb,))
            st, stb = stage.get()
            sds = stds[sti % 2]
            sti += 1
            for ci in reversed(range(len(locs))):
                cl, ncc = locs[ci]
                c = tl[ci]
                cg = self.chbase[seg] + c
                li = self.len_idx(seg, ncc)
                pb_, pbb = self.bfp.get()
                self.transposes([(pb_[0:ncc, h * 128:(h + 1) * 128], kT[:, h, cl:cl + ncc], self.cbv(CB_IDENT)) for h in range(4)],
                                (kTb, self.Bc), (pbb,))
                src = pb_[0:ncc, 0:512].rearrange("p (h d) -> p h d", h=4)
                self.tt("dve", kdf[0:ncc, :, :], src, m["cdF"][0:ncc, li, :].unsqueeze(2).broadcast_to([ncc, 4, 128]), ALU.mult,
                        (pbb, m["Bd"]), (kdfb,))
                self.tt("dve", kdb[0:ncc, :, :], src, m["cdB"][0:ncc, :].unsqueeze(2).broadcast_to([ncc, 4, 128]), ALU.mult,
                        (pbb, m["Bd"]), (kdbb,))
                pF, pFb = self.big.get()
                self.mm_multi([(pF[:, h * 128:(h + 1) * 128], [(kdf[0:ncc, h, :], v4[0:ncc, ci, h * 128:(h + 1) * 128])]) for h in range(4)],
                              (kdfb, v4b), (pFb,))
                pB, pBb = self.big.get()
                self.mm_multi([(pB[:, h * 128:(h + 1) * 128], [(kdb[0:ncc, h, :], v4[0:ncc, ci, h * 128:(h + 1) * 128])]) for h in range(4)],
                              (kdbb, v4b), (pBb,))
                pF3 = pF[:, :].rearrange("p (h e) -> p h e", h=4)
                pB3 = pB[:, :].rearrange("p (h e) -> p h e", h=4)
                tA3 = tA[:, :].rearrange("p (h e) -> p h e", h=4)
                tB3 = tB[:, :].rearrange("p (h e) -> p h e", h=4)
                self.tt("dve", tA3, pF3, m["wF"][:, :, cg:cg + 1].broadcast_to([128, 4, 128]), ALU.mult, (pFb, m["Bd"]), (tAb,))
                self.tt("pool", SfAcc[:], SfAcc[:], tA3, ALU.add, (Sab, tAb), (Sab,))
                self.copy("act", st[:, ci, :], SbRun[:].rearrange("p h e -> p (h e)"), (Srb,), (stb,))
                self.tt("pool", tB3, SbRun[:], m["cB"][:, li, :].unsqueeze(2).broadcast_to([128, 4, 128]), ALU.mult, (Srb, m["Bd"]), (tBb,))
                self.tt("dve", SbRun[:], tB3, pB3, ALU.add, (tBb, pBb), (Srb,))
            cg0 = self.chbase[seg] + tl[0]
            nch_t = len(tl)
            bs = Buf("sbl_%s_%d" % (seg, tl[0]))
            Bsbl[(seg, tl[0])] = bs
            T.dma("sp", lambda: nc.sync.dma_start(out=self.sbl_d[cg0:cg0 + nch_t].rearrange("c p f -> p c f"), in_=st[:, 0:nch_t, :]),
                  sds, (stb,), (bs,))
        if seg != "S":
            return
        Bsend = Buf("send")
        if "sendds" not in self.__dict__:
            self.sendds = T.dsem("send")
            self.ccds = T.dsem("cc")
        dss = self.sendds
        sF = self.sendF.ap()
        sB = self.sendB.ap()
        T.dma("sp", lambda: nc.sync.dma_start(out=sF[:, 0:512], in_=SfAcc[:].rearrange("p h e -> p (h e)")), dss, (Sab,), (Bsend,))
        T.dma("sp", lambda: nc.sync.dma_start(out=sF[:, 512:1024], in_=SbRun[:].rearrange("p h e -> p (h e)")), dss, (Srb,), (Bsend,))
        allka = [m["kaTb"][(seg, ch[tl[0]][0])] for tl in tiles]
        T.dma("sp", lambda: nc.sync.dma_start(out=sB[:, 0:128], in_=m["kaT"][:, 128 + L - 128:128 + L]), dss, allka, (Bsend,))
        T.dma("sp", lambda: nc.sync.dma_start(out=sB[:, 264:392], in_=m["kaT"][:, 128:256]), dss, allka, (Bsend,))
        nfull = len(ch) - 1
        rs_ = ch[-1][1]
        T.dma("sp", lambda: nc.sync.dma_start(out=sB[0:128 - rs_, 128:256], in_=m["va"][rs_:128, 1 + nfull - 1, :]), dss,
              (m["vab"][(seg, nfull - 1)],), (Bsend,))
        T.dma("sp", lambda: nc.sync.dma_start(out=sB[128 - rs_:128, 128:256], in_=m["va"][0:rs_, 1 + nfull, :]), dss,
              (m["vab"][(seg, nfull)],), (Bsend,))
        T.dma("sp", lambda: nc.sync.dma_start(out=sB[:, 392:520], in_=m["va"][:, 1, :]), dss, (m["vab"][(seg, 0)],), (Bsend,))
        T.dma("sp", lambda: nc.sync.dma_start(out=sB[:, 256:264], in_=m["xe"][:, 8:16]), dss, (m["xeb"],), (Bsend,))
        T.dma("sp", lambda: nc.sync.dma_start(out=sB[:, 520:528], in_=m["xe"][:, 0:8]), dss, (m["xeb"],), (Bsend,))
        Bg = Buf("gath")
        m["Bg"] = Bg
        if self.do_cc:
            cds = self.ccds
            T.collective(lambda: nc.gpsimd.collective_compute("AllGather", ALU.bypass, replica_groups=[list(range(NRANK))],
                                                              ins=[self.sendF.ap().opt()], outs=[self.gathF.ap().opt()]),
                         cds, (Bsend,), (Bg,))
            T.collective(lambda: nc.gpsimd.collective_compute("AllGather", ALU.bypass, replica_groups=[list(range(NRANK))],
                                                              ins=[self.sendB.ap().opt()], outs=[self.gathB.ap().opt()]),
                         cds, (Bsend,), (Bg,))

    def receive(self, l):
        nc = self.nc
        T = self.T
        cfg = self.cfg
        m = self.M
        L = cfg.LS
        Sfin, Sbin, Sinb = m["Sfin"], m["Sbin"], m["Sinb"]
        halob = m["halob"]
        blkF = Slots(self, 2, [128, 1024], F32, "blkF")
        blkB = Slots(self, 2, [128, 528], BF16, "blkB")
        dsF = [T.dsem("blkF%d" % i) for i in range(2)]
        dsB = [T.dsem("blkB%d" % i) for i in range(2)]
        accL = self.sb([128, 264], F32, "accL")
        accR = self.sb([128, 264], F32, "accR")
        accb = Buf("acc")
        tA, tAb = m["tmpA"]
        T.op("pool", lambda: nc.gpsimd.memset(Sfin[:], 0.0), (), (Sinb,))
        T.op("pool", lambda: nc.gpsimd.memset(Sbin[:], 0.0), (), (Sinb,))
        T.op("pool", lambda: nc.gpsimd.memset(accL[:], 0.0), (), (accb,))
        T.op("pool", lambda: nc.gpsimd.memset(accR[:], 0.0), (), (accb,))
        if self.do_cc:
            gF = self.gathF.ap()
            gB = self.gathB.ap()
            tA3 = tA[:, :].rearrange("p (h e) -> p h e", h=4)
            for r in range(NRANK):
                bf, bfb = blkF.get()
                bb, bbb = blkB.get()
                T.dma("sp", lambda: nc.sync.dma_start(out=bf[:], in_=gF[r * 128:(r + 1) * 128, :]), dsF[r % 2], (m["Bg"],), (bfb,))
                T.dma("sp", lambda: nc.sync.dma_start(out=bb[:], in_=gB[r * 128:(r + 1) * 128, :]), dsB[r % 2], (m["Bg"],), (bbb,))
                f3 = bf[:, 0:512].rearrange("p (h e) -> p h e", h=4)
                b3 = bf[:, 512:1024].rearrange("p (h e) -> p h e", h=4)
                self.tt("dve", tA3, f3, m["coF"][:, :, r:r + 1].broadcast_to([128, 4, 128]), ALU.mult, (bfb, m["Bd"]), (tAb,))
                self.tt("dve", Sfin[:], Sfin[:], tA3, ALU.add, (Sinb, tAb), (Sinb,))
                self.tt("dve", tA3, b3, m["coB"][:, :, r:r + 1].broadcast_to([128, 4, 128]), ALU.mult, (bfb, m["Bd"]), (tAb,))
                self.tt("dve", Sbin[:], Sbin[:], tA3, ALU.add, (Sinb, tAb), (Sinb,))
                self.stt(accL[:], bb[:, 0:264], self.prm("Lsel", r, r + 1), accL[:], ALU.mult, ALU.add, (bbb, self.Bc, accb), (accb,))
                self.stt(accR[:], bb[:, 264:528], self.prm("Rsel", r, r + 1), accR[:], ALU.mult, ALU.add, (bbb, self.Bc, accb), (accb,))
        nchs = cfg.NCH_S
        self.copy("act", m["kaT"][:, 0:128], accL[:, 0:128], (accb,), (halob,))
        self.copy("act", m["va"][:, 0, :], accL[:, 128:256], (accb,), (halob,))
        self.copy("act", m["xhe"][:, 0, :], accL[:, 256:264], (accb,), (halob,))
        self.copy("act", m["kaT"][:, 128 + L:128 + L + 128], accR[:, 0:128], (accb,), (halob,))
        self.copy("act", m["va"][:, nchs + 1, :], accR[:, 128:256], (accb,), (halob,))
        self.copy("act", m["xhe"][:, 1, :], accR[:, 256:264], (accb,), (halob,))
        vl = m["tmpB"][0]
        vlb = m["tmpB"][1]
        T.op("dve", lambda: nc.vector.tensor_reduce(out=vl[:, 0:1], in_=self.prm("Lsel"), axis=mybir.AxisListType.X, op=ALU.add), (self.Bc,), (vlb,))
        T.op("dve", lambda: nc.vector.tensor_reduce(out=vl[:, 1:2], in_=self.prm("Rsel"), axis=mybir.AxisListType.X, op=ALU.add), (self.Bc,), (vlb,))
        self.ts("dve", m["onesLR"][:, 0, :], self.cbv(CB_ONES), vl[:, 0:1], ALU.mult, (vlb, self.Bc), (halob,))
        self.ts("dve", m["onesLR"][:, 1, :], self.cbv(CB_ONES), vl[:, 1:2], ALU.mult, (vlb, self.Bc), (halob,))

    def key_blocks(self, seg, c):
        cfg = self.cfg
        ch = self.chunks[seg]
        L = self.seg_len[seg]
        q0, nq = ch[c]
        cands = []
        if seg == "S":
            cands.append((-128, 128, 0, 0, "L"))
        for c2, (k0, nk) in enumerate(ch):
            cands.append((k0, nk, 128 + k0, 1 + c2, "1"))
        if seg == "S":
            cands.append((L, 128, 128 + L, len(ch) + 1, "R"))
        out = []
        for (k0, nk, kcol, vblk, kind) in cands:
            off = k0 - q0
            jj = np.arange(nk)[:, None]
            ii = np.arange(nq)[None, :]
            vis = np.abs(off + jj - ii) <= 128
            if not vis.any():
                continue
            mi = None if vis.all() else cfg.MASK_OFFS.index(off)
            out.append((kcol, nk, vblk, mi, kind, k0))
        return out

    def pass_b(self, l, seg):
        nc = self.nc
        T = self.T
        m = self.M
        T.barrier()
        self.sb_off = self.mix_common_off
        ch = self.chunks[seg]
        tiles = self.tiles[seg]
        parts = self.mix_parts
        sblt = Slots(self, 2, [128, 4, 512], BF16, "sblt")
        sblds = [T.dsem("sblt%d" % i) for i in range(2)]
        sbi = 0
        xh = Slots(self, 2, [128, KC, 2], BF16, "xh")
        qT = Slots(self, 2, [128, 512], BF16, "qT")
        kT = Slots(self, 2, [128, 512], BF16, "kT")
        sg = Slots(self, 2, [128, 512], BF16, "sg")
        vb_ = Slots(self, 2, [128, 4, 128], BF16, "vbf")
        attT = Slots(self, 2, [128, 4, 128], BF16, "attT")
        qf = Slots(self, 2, [128, 512], BF16, "qf")
        qbk = Slots(self, 2, [128, 512], BF16, "qbk")
        kdf = Slots(self, 2, [128, 4, 128], BF16, "kdf")
        SfC = Slots(self, 2, [128, 5, 128], BF16, "SfC")
        SbT = Slots(self, 2, [128, 4, 128], BF16, "SbT")
        Sf = self.sb([128, 4, 128], F32, "Sf")
        Sfb = [Buf("Sf%d" % h) for h in range(4)]
        obf = Slots(self, 1, [128, 512], BF16, "obf")
        o2 = Slots(self, 1, [128, 512], BF16, "o2")
        ftmp = Slots(self, 5, [128, 516], F32, "ftmp")
        mu = var = cen = ccs = u = yy = den = m1 = m2 = ftmp
        alias_off = self.sb_off
        r = self.sb([128, 4, 512], BF16, "r")
        ycv = self.sb([128, 4, 512], BF16, "ycv")
        oa = self.sb([128, 4, 512], BF16, "oa")
        qa = self.sb([128, 4, 512], BF16, "qa")
        merged = self.sb([128, KC, 512], BF16, "merged")
        end_off = self.sb_off
        uhr = self.sb([128, 8, 2], F32, "uhr")
        uh = self.sb([128, 4, 2], F32, "uh")
        pt = Slots(self, 4, [128, 512], BF16, "pt")
        sig = Slots(self, 3, [128, 512], BF16, "sig")
        if seg == "S":
            save = self.sb_off
            self.sb_off = alias_off
            self.receive(l)
            assert self.sb_off <= end_off
            self.sb_off = save
            T.barrier()
        else:
            T.op("pool", lambda: nc.gpsimd.memset(m["xhe"][:], 0.0), (), (m["halob"],))

        gain = self.prm("gnm", l * 8, l * 8 + 8)
        if seg == "S":
            for h in range(4):
                self.copy("pool", Sf[:, h, :], m["Sfin"][:, h, :], (m["Sinb"],), (Sfb[h],))
        else:
            for h in range(4):
                T.op("pool", lambda h=h: nc.gpsimd.memset(Sf[:, h, :], 0.0), (), (Sfb[h],))

        def prep_tile(ti):
            tl = tiles[ti]
            s0, t0, n, locs, uniform = self.tile_info(seg, tl)
            xn, xnb = m["xn"].get()
            self.norm_tile(t0, n, gain, lambda k: xn[:, k, 0:n], xnb, (m["sq"], m["lnv"], m["lnb"]))
            return (xn, xnb)

        cur = prep_tile(0)
        lastcol = self.sb([128, KC], BF16, "lastcol")
        lastb = Buf("lastcol")
        for ti, tl in enumerate(tiles):
            s0, t0, n, locs, uniform = self.tile_info(seg, tl)
            xn, xnb = cur
            nxt = prep_tile(ti + 1) if ti + 1 < len(tiles) else None
            xht, xhb = xh.get()
            if ti == 0:
                self.copy("pool", xht[:, :, 0], m["xhe"][:, 0, :], (m["halob"],), (xhb,))
            else:
                self.copy("pool", xht[:, :, 0], lastcol[:, :], (lastb,), (xhb,))
            if nxt is None:
                self.copy("pool", xht[:, :, 1], m["xhe"][:, 1, :], (m["halob"],), (xhb,))
            else:
                self.copy("pool", xht[:, :, 1], nxt[0][:, :, 0], (nxt[1],), (xhb,))
            self.load_rot(t0, n)
            nch_t = len(tl)
            sb_t, sb_b = sblt.get()
            cg0 = self.chbase[seg] + tl[0]
            T.dma("sp", lambda: nc.sync.dma_start(out=sb_t[:, 0:nch_t, :], in_=self.sbl_d[cg0:cg0 + nch_t].rearrange("c p f -> p c f")),
                  sblds[sbi % 2], (m["Bsbl"][(seg, tl[0])],), (sb_b,))
            sbi += 1
            rbuf = Buf("r")
            ycvb = Buf("ycv")
            oab = Buf("oa")
            qab = Buf("qa")

            if "ret" in parts:
                for h in range(4):
                    wt, wtb = self.ring_next(("rh", l, h))
                    wv = wt[:, 0:4096].rearrange("p (k c) -> p k c", k=8)
                    qTt, qTb = qT.get()
                    kTt, kTb_ = kT.get()
                    sgt, sgb = sg.get()
                    vt, vtb = vb_.get()
                    ps, psb = self.big.get()
                    self.mm(ps[:, 0:n], [(wv[:, k, 0:128], xn[:, k, 0:n]) for k in range(KC)], (wtb, xnb), (psb,))
                    self.rotary(ps[:, 0:n], psb, n, "r", qTt[:, 0:n], qTb)
                    ps, psb = self.big.get()
                    self.mm(ps[:, 0:n], [(wv[:, k, 128:256], xn[:, k, 0:n]) for k in range(KC)], (wtb, xnb), (psb,))
                    self.rotary(ps[:, 0:n], psb, n, "r", kTt[:, 0:n], kTb_)
                    ps, psb = self.big.get()
                    self.mm(ps[:, 0:n], [(wv[:, k, 384:512], xn[:, k, 0:n]) for k in range(KC)], (wtb, xnb), (psb,))
                    self.act(sgt[:, 0:n], ps[:, 0:n], AF.Silu, (psb,), (sgb,))
                    ps, psb = self.big.get()
                    self.mm_multi([(ps[0:ncc, ci * 128:(ci + 1) * 128], [(xn[:, k, cl:cl + ncc], wv[:, k, 256:384]) for k in range(KC)])
                                   for ci, (cl, ncc) in enumerate(locs)], (wtb, xnb), (psb,))
                    nmax = max(c_[1] for c_ in locs)
                    self.copy("act", vt[0:nmax, 0:nch_t, :], ps[0:nmax, 0:nch_t * 128].rearrange("p (c e) -> p c e", c=nch_t), (psb,), (vtb,))
                    pss, pssb = self.big.get()
                    self.mm_multi([(pss[0:ncc, ci * 128:ci * 128 + ncc], [(kTt[:, cl:cl + ncc], qTt[:, cl:cl + ncc])])
                                   for ci, (cl, ncc) in enumerate(locs)], (kTb_, qTb), (pssb,))
                    at, atb = attT.get()
                    qft, qfb = qf.get()
                    qbt, qbb = qbk.get()
                    for ci, (cl, ncc) in enumerate(locs):
                        li = self.len_idx(seg, ncc)
                        self.tt("dve", at[0:ncc, ci, 0:ncc], pss[0:ncc, ci * 128:ci * 128 + ncc], m["Dc"][0:ncc, h, 0:ncc], ALU.mult,
                                (pssb, m["Bd"]), (atb,))
                        self.tt("pool", qft[:, cl:cl + ncc], qTt[:, cl:cl + ncc], m["rowF"][:, h, 0:ncc], ALU.mult, (qTb, m["Bd"]), (qfb,))
                        self.tt("pool", qbt[:, cl:cl + ncc], qTt[:, cl:cl + ncc], m["rowB"][:, li, h, 0:ncc], ALU.mult, (qTb, m["Bd"]), (qbb,))
                    pk, pkb = self.bfp.get()
                    self.transposes([(pk[0:ncc, ci * 128:(ci + 1) * 128], kTt[:, cl:cl + ncc], self.cbv(CB_IDENT)) for ci, (cl, ncc) in enumerate(locs)],
                                    (kTb_, self.Bc), (pkb,))
                    kd, kdb_ = kdf.get()
                    for ci, (cl, ncc) in enumerate(locs):
                        li = self.len_idx(seg, ncc)
                        self.ts("dve", kd[0:ncc, ci, :], pk[0:ncc, ci * 128:(ci + 1) * 128], m["cdF"][0:ncc, li, h:h + 1], ALU.mult,
                                (pkb, m["Bd"]), (kdb_,))
                    pd, pdb = self.big.get()
                    self.mm_multi([(pd[:, ci * 128:(ci + 1) * 128], [(kd[0:ncc, ci, :], vt[0:ncc, ci, :])]) for ci, (cl, ncc) in enumerate(locs)],
                                  (kdb_, vtb), (pdb,))
                    sfc, sfcb = SfC.get()
                    sbt, sbtb = SbT.get()
                    self.copy("act", sfc[:, 0, :], Sf[:, h, :], (Sfb[h],), (sfcb,))
                    for ci, (cl, ncc) in enumerate(locs):
                        li = self.len_idx(seg, ncc)
                        cg = self.chbase[seg] + tl[ci]
                        self.stt(Sf[:, h, :], Sf[:, h, :], m["cF"][:, li, h:h + 1], pd[:, ci * 128:(ci + 1) * 128], ALU.mult, ALU.add,
                                 (Sfb[h], pdb, m["Bd"]), (Sfb[h],))
                        self.copy("act", sfc[:, ci + 1, :], Sf[:, h, :], (Sfb[h],), (sfcb,))
                        if seg == "S":
                            self.stt(sbt[:, ci, :], m["Sbin"][:, h, :], m["wB"][:, h, cg:cg + 1], sb_t[:, ci, h * 128:(h + 1) * 128], ALU.mult, ALU.add,
                                     (m["Sinb"], m["Bd"], sb_b), (sbtb,))
                        else:
                            self.copy("pool", sbt[:, ci, :], sb_t[:, ci, h * 128:(h + 1) * 128], (sb_b,), (sbtb,))
                    po, pob = self.big.get()
                    self.mm_multi([(po[:, cl:cl + ncc], [(vt[0:ncc, ci, :], at[0:ncc, ci, 0:ncc]),
                                                         (sfc[:, ci, :], qft[:, cl:cl + ncc]),
                                                         (sbt[:, ci, :], qbt[:, cl:cl + ncc])]) for ci, (cl, ncc) in enumerate(locs)],
                                  (vtb, atb, sfcb, qfb, sbtb, qbb), (pob,))
                    ob_, obb = obf.get()
                    o2_, o2b = o2.get()
                    self.copy("act", ob_[:, 0:n], po[:, 0:n], (pob,), (obb,))
                    self.act(o2_[:, 0:n], po[:, 0:n], AF.Square, (pob,), (o2b,))
                    p1, p1b = self.big.get()
                    self.mm(p1[:, 0:n], [(self.cbv(CB_OGN), ob_[:, 0:n])], (obb, self.Bc), (p1b,))
                    p2, p2b = self.big.get()
                    self.mm(p2[:, 0:n], [(self.cbv(CB_OGN), o2_[:, 0:n])], (o2b, self.Bc), (p2b,))
                    mu_, mub = mu.get()
                    var_, varb = var.get()
                    cen_, cenb = cen.get()
                    self.copy("act", mu_[:, 0:n], p1[:, 0:n], (p1b,), (mub,))
                    self.tt("dve", var_[:, 0:n], mu_[:, 0:n], mu_[:, 0:n], ALU.mult, (mub,), (varb,))
                    self.stt(var_[:, 0:n], var_[:, 0:n], -1.0, p2[:, 0:n], ALU.mult, ALU.add, (varb, p2b), (varb,))
                    self.ts("dve", var_[:, 0:n], var_[:, 0:n], 0.0, ALU.max, (varb,), (varb,))
                    self.act(var_[:, 0:n], var_[:, 0:n], AF.Ln, (varb, self.Bc), (varb,), bias=self.epsc[:, 0:1])
                    self.act(var_[:, 0:n], var_[:, 0:n], AF.Exp, (varb,), (varb,), scale=-0.5)
                    self.tt("dve", cen_[:, 0:n], po[:, 0:n], mu_[:, 0:n], ALU.subtract, (pob, mub), (cenb,))
                    self.stt(cen_[:, 0:n], cen_[:, 0:n], self.prm("gng", l * 4 + h, l * 4 + h + 1), var_[:, 0:n], ALU.mult, ALU.mult,
                             (cenb, varb, self.Bc), (cenb,))
                    self.tt("pool", r[:, h, 0:n], cen_[:, 0:n], sgt[:, 0:n], ALU.mult, (cenb, sgb), (rbuf,))

            if "conv" in parts:
                wcc, wccb = self.ring_next(("cc", l))
                wcx, wcxb = self.ring_next(("cx", l), keep=1)
                wcb, wcbb = self.ring_next(("cb", l), keep=2)
                vcc = wcc[:, 0:4096].rearrange("p (k c) -> p k c", k=8)
                vcx = wcx[:, 0:4096].rearrange("p (k c) -> p k c", k=8)
                vcb = wcb[:, 0:4096].rearrange("p (k c) -> p k c", k=8)
                ph, phb = self.big.get()
                self.mm_multi([(ph[:, (wi * 4 + mm_) * 2:(wi * 4 + mm_) * 2 + 2], [(wvw[:, k, mm_ * 128:(mm_ + 1) * 128], xht[:, k, :]) for k in range(KC)])
                               for wi, wvw in enumerate((vcc, vcx)) for mm_ in range(4)], (wccb, wcxb, xhb), (phb,))
                uhb = Buf("uh")
                self.copy("act", uhr[:].rearrange("p a b -> p (a b)"), ph[:, 0:16], (phb,), (uhb,))
                self.tt("dve", uh[:], uhr[:, 0:4, :], uhr[:, 4:8, :], ALU.mult, (uhb,), (uhb,))
                for mm_ in range(4):
                    cs, csb = ccs.get()
                    ut, utb = u.get()
                    yt, ytb = yy.get()
                    ps, psb = self.big.get()
                    self.mm(ps[:, 0:n], [(vcc[:, k, mm_ * 128:(mm_ + 1) * 128], xn[:, k, 0:n]) for k in range(KC)], (wccb, xnb), (psb,))
                    self.copy("act", cs[:, 0:n], ps[:, 0:n], (psb,), (csb,))
                    ps, psb = self.big.get()
                    self.mm(ps[:, 0:n], [(vcx[:, k, mm_ * 128:(mm_ + 1) * 128], xn[:, k, 0:n]) for k in range(KC)], (wcxb, xnb), (psb,))
                    self.tt("dve", ut[:, 1:n + 1], ps[:, 0:n], cs[:, 0:n], ALU.mult, (psb, csb), (utb,))
                    self.copy("pool", ut[:, 0:1], uh[:, mm_, 0:1], (uhb,), (utb,))
                    self.copy("pool", ut[:, n + 1:n + 2], uh[:, mm_, 1:2], (uhb,), (utb,))
                    cwo = l * 12
                    self.ts("pool", yt[:, 0:n], ut[:, 0:n], self.prm("cw", cwo + 0 * 4 + mm_, cwo + 0 * 4 + mm_ + 1), ALU.mult, (utb, self.Bc), (ytb,))
                    self.stt(yt[:, 0:n], ut[:, 1:n + 1], self.prm("cw", cwo + 1 * 4 + mm_, cwo + 1 * 4 + mm_ + 1), yt[:, 0:n], ALU.mult, ALU.add,
                             (utb, ytb, self.Bc), (ytb,))
                    self.stt(yt[:, 0:n], ut[:, 2:n + 2], self.prm("cw", cwo + 2 * 4 + mm_, cwo + 2 * 4 + mm_ + 1), yt[:, 0:n], ALU.mult, ALU.add,
                             (utb, ytb, self.Bc), (ytb,))
                    ps, psb = self.big.get()
                    self.mm(ps[:, 0:n], [(vcb[:, k, mm_ * 128:(mm_ + 1) * 128], xn[:, k, 0:n]) for k in range(KC)], (wcbb, xnb), (psb,))
                    self.tt("dve", ycv[:, mm_, 0:n], ps[:, 0:n], yt[:, 0:n], ALU.mult, (psb, ytb), (ycvb,))

            if "att" in parts:
                wq, wqb = self.ring_next(("aq", l))
                vq = wq[:, 0:4096].rearrange("p (k c) -> p k c", k=8)
                for j in range(4):
                    ps, psb = self.big.get()
                    self.mm(ps[:, 0:n], [(vq[:, k, j * 128:(j + 1) * 128], xn[:, k, 0:n]) for k in range(KC)], (wqb, xnb), (psb,))
                    self.rotary(ps[:, 0:n], psb, n, "a", qa[:, j, 0:n], qab)
                for ci, (cl, ncc) in enumerate(locs):
                    c = tl[ci]
                    blocks = self.key_blocks(seg, c)
                    for kv in range(2):
                        p0 = 64 * kv
                        pts = []
                        for (kcol, nk, vblk, mi, kind, k0) in blocks:
                            if kind == "1":
                                tstart = None
                                for tl2 in tiles:
                                    s2 = ch[tl2[0]][0]
                                    e2 = ch[tl2[-1]][0] + ch[tl2[-1]][1]
                                    if s2 <= k0 < e2:
                                        tstart = s2
                                kbuf = m["kaTb"][(seg, tstart)]
                                vbuf = m["vab"][(seg, vblk - 1)]
                                ones_ap = self.cbv(CB_ONES, rows=nk)
                            else:
                                kbuf = m["halob"]
                                vbuf = m["halob"]
                                ones_ap = m["onesLR"][0:nk, 0 if kind == "L" else 1, :]
                            sc, scb = self.big.get()
                            self.mm(sc[0:nk, 0:4 * ncc], [(m["kaT"][p0:p0 + 64, kcol:kcol + nk], qa[p0:p0 + 64, :, cl:cl + ncc])], (kbuf, qab), (scb,))
                            pt_, ptb = pt.get()
                            self.act(pt_[0:nk, 0:4 * ncc], sc[0:nk, 0:4 * ncc], AF.Exp, (scb,), (ptb,), scale=0.125)
                            if mi is not None:
                                p3 = pt_[0:nk, 0:4 * ncc].rearrange("p (j i) -> p j i", j=4)
                                mk = self.cbv(CB_MASK0 + mi, rows=nk, cols=ncc).unsqueeze(1).broadcast_to([nk, 4, ncc])
                                self.tt("pool", p3, p3, mk, ALU.mult, (ptb, self.Bc), (ptb,))
                            pts.append((pt_, ptb, nk, vblk, ones_ap, vbuf))
                        po, pob = self.big.get()
                        self.mm(po[:, 0:4 * ncc], [(m["va"][0:nk, vblk, :], pt_[0:nk, 0:4 * ncc]) for (pt_, ptb, nk, vblk, oap, vbuf) in pts],
                                [x_[1] for x_ in pts] + [x_[5] for x_ in pts], (pob,))
                        pdn, pdnb = self.big.get()
                        self.mm(pdn[:, 0:4 * ncc], [(oap, pt_[0:nk, 0:4 * ncc]) for (pt_, ptb, nk, vblk, oap, vbuf) in pts],
                                [x_[1] for x_ in pts] + [self.Bc, m["halob"]], (pdnb,))
                        dn, dnb = den.get()
                        d3 = dn[p0:p0 + 64, 0:4 * ncc].rearrange("p (j i) -> p j i", j=4)
                        self.tt("dve", d3, pdn[p0:p0 + 64, 0:4 * ncc].rearrange("p (j i) -> p j i", j=4),
                                self.esink[p0:p0 + 64, l * 4:(l + 1) * 4].unsqueeze(2).broadcast_to([64, 4, ncc]), ALU.add, (pdnb, self.Bc), (dnb,))
                        T.op("dve", lambda: nc.vector.reciprocal(out=dn[p0:p0 + 64, 0:4 * ncc], in_=dn[p0:p0 + 64, 0:4 * ncc]), (dnb,), (dnb,))
                        self.tt("dve", oa[p0:p0 + 64, :, cl:cl + ncc], po[p0:p0 + 64, 0:4 * ncc].rearrange("p (j i) -> p j i", j=4), d3, ALU.mult,
                                (pob, dnb), (oab,))

            mgb = Buf("merged")
            for mo in range(8):
                wg, wgb = self.ring_next(("gg", l, mo))
                wo_, wob = self.ring_next(("go", l, mo), keep=1)
                vg = wg[:, 0:3072].rearrange("p (k c) -> p k c", k=8)
                vo = wo_[:, 0:1536].rearrange("p (k c) -> p k c", k=12)
                m1t, m1b = m1.get()
                m2t, m2b = m2.get()
                first = True
                for b, (pname, src, sbuf_) in enumerate((("ret", r, rbuf), ("conv", ycv, ycvb), ("att", oa, oab))):
                    if pname not in parts:
                        continue
                    ps, psb = self.big.get()
                    self.mm(ps[:, 0:n], [(vg[:, k, b * 128:(b + 1) * 128], xn[:, k, 0:n]) for k in range(KC)], (wgb, xnb), (psb,))
                    sgt_, sgtb = sig.get()
                    self.act(sgt_[:, 0:n], ps[:, 0:n], AF.Sigmoid, (psb,), (sgtb,))
                    ps, psb = self.big.get()
                    self.mm(ps[:, 0:n], [(vo[:, b * 4 + kk, :], src[:, kk, 0:n]) for kk in range(4)], (wob, sbuf_), (psb,))
                    if first:
                        self.tt("dve", m1t[:, 0:n], ps[:, 0:n], sgt_[:, 0:n], ALU.mult, (psb, sgtb), (m1b,))
                        first = False
                    else:
                        self.tt("dve", m2t[:, 0:n], ps[:, 0:n], sgt_[:, 0:n], ALU.mult, (psb, sgtb), (m2b,))
                        self.tt("pool", m1t[:, 0:n], m1t[:, 0:n], m2t[:, 0:n], ALU.add, (m1b, m2b), (m1b,))
                self.copy("act", merged[:, mo, 0:n], m1t[:, 0:n], (m1b,), (mgb,))
            for hf in range(2):
                ww, wwb = self.ring_next(("wo", l, hf))
                vw = ww[:, 0:4096].rearrange("p (k c) -> p k c", k=8)
                for mq in range(4):
                    mo = hf * 4 + mq
                    ps, psb = self.big.get()
                    self.mm(ps[:, 0:n], [(vw[:, k, mq * 128:(mq + 1) * 128], merged[:, k, 0:n]) for k in range(KC)], (wwb, mgb), (psb,))
                    self.resid_add(mo, t0, n, ps[:, 0:n], psb, 1.0)
            self.copy("pool", lastcol[:, :], xn[:, :, n - 1], (xnb,), (lastb,))
            cur = nxt


_PROG_CACHE = {}


def get_prog(**kw):
    key = tuple(sorted((k, str(v)) for k, v in kw.items()))
    if key not in _PROG_CACHE:
        _PROG_CACHE[key] = Prog(**kw)
    return _PROG_CACHE[key]


WNAMES = ["w_ffn1_gate", "w_ffn1_up", "w_ffn1_down", "w_in", "w_ret_out", "w_conv_out", "w_attn_out", "w_o",
          "w_ffn2_gate", "w_ffn2_up", "w_ffn2_down"]


def make_in_maps(cfg, inp):
    meta = np.asarray(inp["meta_tokens"], np.float32)
    xp = np.asarray(inp["x_prompt"], np.float32)
    xs = np.asarray(inp["x_sample"], np.float32)
    ws = {k: np.ascontiguousarray(np.asarray(inp[k], np.float32)) for k in WNAMES}
    small = {k: np.asarray(inp[k], np.float32) for k in ("norm_ffn1", "norm_mix", "norm_ffn2", "final_norm", "ret_gn_gain",
                                                        "conv_w", "attn_sink", "ret_decay")}
    maps = []
    for c in range(NCORES):
        b = c // 4
        s = c % 4
        full = np.concatenate([meta, xs[b]], 0)
        xin = np.concatenate([meta, xp[c], full[s * cfg.LS:(s + 1) * cfg.LS]], 0)
        P, B = host_consts(cfg, c, small)
        mp = {"x": np.ascontiguousarray(xin), "prm": P, "cstb": B, "rot": host_rotary(cfg, c)}
        mp.update(ws)
        maps.append(mp)
    return maps


def assemble(cfg, res):
    SP = cfg.LP - NMETA
    SS = 4 * cfg.LS - NMETA
    yp = np.zeros((8, SP, D), np.float32)
    ys = np.zeros((2, SS, D), np.float32)
    for c in range(NCORES):
        y = res[c]["y"]
        yp[c] = y[NMETA:cfg.LP]
        b = c // 4
        s = c % 4
        seg = y[cfg.LP:]
        lo = s * cfg.LS
        g0 = max(lo, NMETA)
        ys[b, g0 - NMETA: lo + cfg.LS - NMETA] = seg[g0 - lo:]
    return yp, ys


def kernel(**inputs):
    cfg = Cfg()
    prog = get_prog()
    maps = make_in_maps(cfg, inputs)
    res = run_bass_kernel_spmd(prog.nc, maps, core_ids=list(range(NCORES)))
    return assemble(cfg, res.results)
```
